# Optimizing a Trainium2 kernel written in Bass

```python
import math
import jax, jax.numpy as jnp
from jax import lax
import numpy as np

D_MODEL = 2048
BATCH = 1
SEQ = 8192
DEPTH = 2

N_META = 16
F_WIDTH = D_MODEL // 2
F_GROUPS = 4
F_GROUP = F_WIDTH // F_GROUPS
N_HEADS = 16
QK_NOPE = 128
QK_ROPE = 64
QK_HEAD = QK_NOPE + QK_ROPE
V_HEAD = 128
Q_LORA = 768
KV_LORA = 512
ROPE_THETA = 10000.0
C_WIDTH = D_MODEL // 2
N_BRANCH = 3
D_FF = 5632
Q_BLOCK = 128
EPS = 1e-6

OFF_Q = F_WIDTH
OFF_KV = OFF_Q + Q_LORA
OFF_KR = OFF_KV + KV_LORA
OFF_C = OFF_KR + QK_ROPE
OFF_G = OFF_C + 3 * C_WIDTH
N_IN = OFF_G + N_BRANCH * D_MODEL

kernel_name = "hybrid_fnet_mla_shortconv_convffn_encoder"


def _rms(x, g):
    xf = x.astype(jnp.float32)
    y = xf * lax.rsqrt(jnp.mean(xf * xf, axis=-1, keepdims=True) + EPS)
    return (y * g.astype(jnp.float32)).astype(x.dtype)


def _dwconv3(h, w):
    hp = jnp.pad(h, ((0, 0), (1, 1), (0, 0)))
    return hp[:, :-2] * w[0] + hp[:, 1:-1] * w[1] + hp[:, 2:] * w[2]


def _rope_tables(L):
    inv = 1.0 / (ROPE_THETA ** (jnp.arange(0, QK_ROPE, 2, dtype=jnp.float32) / QK_ROPE))
    ang = jnp.arange(L, dtype=jnp.float32)[:, None] * inv[None, :]
    return jnp.cos(ang)[None, :, None, :], jnp.sin(ang)[None, :, None, :]


def _apply_rope(x, cos, sin):
    xf = x.astype(jnp.float32)
    half = QK_ROPE // 2
    x1, x2 = xf[..., :half], xf[..., half:]
    out = jnp.concatenate([x1 * cos - x2 * sin, x2 * cos + x1 * sin], axis=-1)
    return out.astype(x.dtype)


def _attend(q, k, v):
    s = jnp.einsum('bqhd,bkhd->bhqk', q, k).astype(jnp.float32) * (1.0 / math.sqrt(QK_HEAD))
    p = jax.nn.softmax(s, axis=-1)
    return jnp.einsum('bhqk,bkhd->bqhd', p.astype(v.dtype), v)


def _dense_attention(q, k, v):
    B = q.shape[0]
    out_meta = _attend(q[:, :N_META], k, v)
    q_real = q[:, N_META:]
    nb = q_real.shape[1] // Q_BLOCK
    qb = q_real.reshape(B, nb, Q_BLOCK, N_HEADS, QK_HEAD).transpose(1, 0, 2, 3, 4)
    ob = lax.map(lambda qq: _attend(qq, k, v), qb)
    out_real = ob.transpose(1, 0, 2, 3, 4).reshape(B, nb * Q_BLOCK, N_HEADS, V_HEAD)
    return jnp.concatenate([out_meta, out_real], axis=1)


def _mixer(xn, cos, sin, w_in, g_qa, g_kva, w_uq, w_ukv, g_q, g_k, conv_c,
           w_pa, w_pb, w_pc, w_o):
    B, L, _ = xn.shape
    p = xn @ w_in
    a = p[..., :OFF_Q]
    cq = p[..., OFF_Q:OFF_KV]
    ckv = p[..., OFF_KV:OFF_KR]
    k_rope = p[..., OFF_KR:OFF_C]
    cb, cc, ch = jnp.split(p[..., OFF_C:OFF_G], 3, axis=-1)
    gates = p[..., OFF_G:]

    fa = jnp.fft.fft2(a.reshape(B, L, F_GROUPS, F_GROUP).astype(jnp.float32),
                      axes=(1, 3), norm='ortho').real
    ya = fa.reshape(B, L, F_WIDTH).astype(xn.dtype) @ w_pa

    q = (_rms(cq, g_qa) @ w_uq).reshape(B, L, N_HEADS, QK_HEAD)
    kv = (_rms(ckv, g_kva) @ w_ukv).reshape(B, L, N_HEADS, QK_NOPE + V_HEAD)
    k_nope, v = kv[..., :QK_NOPE], kv[..., QK_NOPE:]
    k_r = jnp.broadcast_to(k_rope[:, :, None, :], (B, L, N_HEADS, QK_ROPE))
    k = jnp.concatenate([k_nope, k_r], axis=-1)
    q = _rms(q, g_q)
    k = _rms(k, g_k)
    q = jnp.concatenate([q[..., :QK_NOPE], _apply_rope(q[..., QK_NOPE:], cos, sin)], axis=-1)
    k = jnp.concatenate([k[..., :QK_NOPE], _apply_rope(k[..., QK_NOPE:], cos, sin)], axis=-1)
    o = _dense_attention(q, k, v)
    yb = o.reshape(B, L, N_HEADS * V_HEAD) @ w_pb

    yc = (cb * _dwconv3(cc * ch, conv_c)) @ w_pc

    g = jax.nn.sigmoid(gates.astype(jnp.float32)).astype(xn.dtype).reshape(B, L, N_BRANCH, D_MODEL)
    merged = g[..., 0, :] * ya + g[..., 1, :] * yb + g[..., 2, :] * yc
    return merged @ w_o


def _conv_ffn(xn, w_up, conv_ffn, w_down):
    h = _dwconv3(xn @ w_up, conv_ffn)
    a, b = h[..., :D_FF], h[..., D_FF:]
    return (jax.nn.silu(a) * b) @ w_down


def setup_inputs(seed: int = 0) -> dict:
    key = jax.random.key(seed)
    ks = jax.random.split(key, 20)
    f32 = jnp.float32

    def nrm(k, shape, scale):
        return jax.random.normal(k, shape, f32) * scale

    def gain(k, shape):
        return 1.0 + 0.01 * jax.random.normal(k, shape, f32)

    return {
        "x": nrm(ks[0], (BATCH, SEQ, D_MODEL), 1.0),
        "meta_tokens": nrm(ks[1], (N_META, D_MODEL), 1.0),
        "g_mix": gain(ks[2], (DEPTH, D_MODEL)),
        "w_in": nrm(ks[3], (DEPTH, D_MODEL, N_IN), D_MODEL ** -0.5),
        "g_qa": gain(ks[4], (DEPTH, Q_LORA)),
        "g_kva": gain(ks[5], (DEPTH, KV_LORA)),
        "w_uq": nrm(ks[6], (DEPTH, Q_LORA, N_HEADS * QK_HEAD), Q_LORA ** -0.5),
        "w_ukv": nrm(ks[7], (DEPTH, KV_LORA, N_HEADS * (QK_NOPE + V_HEAD)), KV_LORA ** -0.5),
        "g_q": gain(ks[8], (DEPTH, QK_HEAD)),
        "g_k": gain(ks[9], (DEPTH, QK_HEAD)),
        "conv_c": nrm(ks[10], (DEPTH, 3, C_WIDTH), 3 ** -0.5),
        "w_pa": nrm(ks[11], (DEPTH, F_WIDTH, D_MODEL), F_WIDTH ** -0.5),
        "w_pb": nrm(ks[12], (DEPTH, N_HEADS * V_HEAD, D_MODEL), (N_HEADS * V_HEAD) ** -0.5),
        "w_pc": nrm(ks[13], (DEPTH, C_WIDTH, D_MODEL), C_WIDTH ** -0.5),
        "w_o": nrm(ks[14], (DEPTH, D_MODEL, D_MODEL), D_MODEL ** -0.5),
        "g_ffn": gain(ks[15], (DEPTH, D_MODEL)),
        "w_up": nrm(ks[16], (DEPTH, D_MODEL, 2 * D_FF), D_MODEL ** -0.5),
        "conv_ffn": nrm(ks[17], (DEPTH, 3, 2 * D_FF), 3 ** -0.5),
        "w_down": nrm(ks[18], (DEPTH, D_FF, D_MODEL), D_FF ** -0.5),
    }


def reference(x, meta_tokens, g_mix, w_in, g_qa, g_kva, w_uq, w_ukv, g_q, g_k, conv_c,
              w_pa, w_pb, w_pc, w_o, g_ffn, w_up, conv_ffn, w_down):
    B = x.shape[0]
    meta = jnp.broadcast_to(meta_tokens[None].astype(x.dtype), (B, N_META, D_MODEL))
    h = jnp.concatenate([meta, x], axis=1)
    cos, sin = _rope_tables(h.shape[1])
    for l in range(DEPTH):
        h = h + _mixer(_rms(h, g_mix[l]), cos, sin, w_in[l], g_qa[l], g_kva[l],
                       w_uq[l], w_ukv[l], g_q[l], g_k[l], conv_c[l],
                       w_pa[l], w_pb[l], w_pc[l], w_o[l])
        h = h + _conv_ffn(_rms(h, g_ffn[l]), w_up[l], conv_ffn[l], w_down[l])
    return h[:, N_META:]
```

```python
import math
import numpy as np
import ml_dtypes
import concourse.bass as bass
import concourse.mybir as mybir
from concourse.bass_utils import run_bass_kernel_spmd

F32 = mybir.dt.float32
BF16 = mybir.dt.bfloat16
AF = mybir.ActivationFunctionType
ALU = mybir.AluOpType
NPBF = ml_dtypes.bfloat16

NCORE = 8
D = 2048
KC = 16
SEQ = 8192
NMETA = 16
L = SEQ + NMETA
T = L // NCORE
HALO = 3
TX = T + 2 * HALO
TILES = [(0, 344), (344, 344), (688, 344)]
NKC = (L + 127) // 128
QT = 512
NQT = (L + QT - 1) // QT
F_WIDTH = 1024
Q_LORA = 768
KV_LORA = 512
OFF_Q = 1024
OFF_KV = 1792
OFF_KR = 2304
OFF_C = 2368
OFF_G = 5440
N_IN = 11584
D_FF = 5632
EPS = 1e-6
SB_BASE = 16512
SB_TOP = 229344


class Res:
    __slots__ = ("w", "r", "excl")

    def __init__(self, excl=False):
        self.w = None
        self.r = []
        self.excl = excl


class _Op:
    __slots__ = ("eng", "calls", "deps", "idx", "stream", "ninc", "marked", "count")


class _Rec:
    def __init__(self):
        self.calls = []

    def __getattr__(self, name):
        def f(*a, **kw):
            self.calls.append((name, a, kw))
            return self
        return f


class Prog:
    ENGS = ["pe", "act", "dve", "pool", "sp"]

    def __init__(self, nc):
        self.nc = nc
        self.ops = []
        self.last_on_stream = {}
        self.streams = []
        self.bar = set()
        self.bar_pending = set()

    def stream(self, name):
        name = "%s%d" % (name, len(self.streams))
        self.streams.append(name)
        return name

    def barrier(self):
        last = {}
        for op in self.ops:
            key = ("s", op.stream) if op.stream is not None else ("e", op.eng)
            last[key] = op.idx
        self.bar = set(last.values())
        self.bar_pending = set(self.ENGS)

    def add(self, eng, fn, reads=(), writes=(), stream=None, ninc=1):
        if _OP_LIMIT is not None and len(self.ops) >= _OP_LIMIT:
            return None
        op = _Op()
        op.eng = eng
        rec = _Rec()
        fn(rec)
        op.calls = rec.calls
        assert len(op.calls) == (ninc if stream is not None else 1)
        op.idx = len(self.ops)
        op.stream = stream
        op.ninc = ninc
        op.marked = False
        op.count = None
        deps = set()
        if any(r.excl for r in reads):
            writes = list(writes) + [r for r in reads if r.excl]
            reads = [r for r in reads if not r.excl]
        if eng in self.bar_pending:
            deps |= self.bar
            self.bar_pending.discard(eng)
        for r in reads:
            if r.w is not None:
                deps.add(r.w)
        for w in writes:
            if w.w is not None:
                deps.add(w.w)
            deps.update(w.r)
        if stream is not None:
            if stream in self.last_on_stream:
                deps.add(self.last_on_stream[stream])
            self.last_on_stream[stream] = op.idx
        op.deps = deps
        for r in reads:
            r.r.append(op.idx)
        for w in writes:
            w.w = op.idx
            w.r = []
        self.ops.append(op)
        return op.idx

    def emit(self, final_wait_streams=()):
        import contextlib
        nc = self.nc
        ops = self.ops
        for op in ops:
            nd = set()
            for d in op.deps:
                p = ops[d]
                if p.stream is None and p.eng == "pe" and op.eng == "pe" and op.stream is None:
                    continue
                nd.add(d)
            op.deps = nd
            for d in nd:
                ops[d].marked = True
        ecount = {e: 0 for e in self.ENGS}
        scount = {s: 0 for s in self.streams}
        for op in ops:
            if op.stream is not None:
                scount[op.stream] += 16 * op.ninc
                op.count = scount[op.stream]
            elif op.marked:
                ecount[op.eng] += 1
                op.count = ecount[op.eng]
        with contextlib.ExitStack() as es:
            esem = {e: es.enter_context(nc.semaphore("e_" + e)) for e in self.ENGS}
            ssem = {s: es.enter_context(nc.semaphore("s_" + s)) for s in self.streams}
            block = es.enter_context(nc.Block())

            def run_engine(ename, eng):
                seen = {}
                for op in ops:
                    if op.eng != ename:
                        continue
                    need = {}
                    for d in op.deps:
                        p = ops[d]
                        key = ("s", p.stream) if p.stream is not None else ("e", p.eng)
                        if p.count > need.get(key, 0):
                            need[key] = p.count
                    for key, val in need.items():
                        if seen.get(key, 0) >= val:
                            continue
                        seen[key] = val
                        sem = ssem[key[1]] if key[0] == "s" else esem[key[1]]
                        eng.wait_ge(sem, val)
                    ins = [getattr(eng, nm)(*a, **kw) for (nm, a, kw) in op.calls]
                    if op.stream is not None:
                        for i_ in ins:
                            i_.then_inc(ssem[op.stream], 16)
                    elif op.marked:
                        ins[-1].then_inc(esem[ename], 1)
                if ename == "sp":
                    for s in final_wait_streams:
                        if scount[s] > 0:
                            eng.wait_ge(ssem[s], scount[s])

            @block.tensor
            def _(eng):
                run_engine("pe", eng)

            @block.scalar
            def _(eng):
                run_engine("act", eng)

            @block.vector
            def _(eng):
                run_engine("dve", eng)

            @block.gpsimd
            def _(eng):
                run_engine("pool", eng)

            @block.sync
            def _(eng):
                run_engine("sp", eng)


class Buf:
    __slots__ = ("t", "r")

    def __init__(self, t, r=None):
        self.t = t
        self.r = r if r is not None else Res()


class KB:
    def __init__(self, name):
        self.nc = bass.Bass("TRN2", target_bir_lowering=False)
        self.p = Prog(self.nc)
        self.top = SB_BASE
        self.n = 0
        self.banks = [Buf(self.nc.alloc_psum_tensor("psb%d" % i, [128, 512], F32), Res(excl=True)) for i in range(8)]
        self.bank_i = 0
        self.rot = {}

    def dram(self, name, shape, dt, kind):
        return self.nc.dram_tensor(name, list(shape), dt, kind=kind).ap()

    def alloc(self, shape, dt, name="t"):
        sz = 4 if dt == F32 else 2
        nb = sz
        for s in shape[1:]:
            nb *= s
        off = self.top
        self.top = (off + nb + 31) // 32 * 32
        assert self.top <= SB_TOP, ("SBUF overflow", name, self.top)
        self.n += 1
        return Buf(self.nc.alloc_sbuf_tensor_at("%s_%d" % (name, self.n), list(shape), dt, offset=off))

    def mark(self):
        return self.top

    def release(self, m):
        self.p.barrier()
        self.top = m

    def bank(self, subset=None):
        subset = subset or list(range(8))
        b = subset[self.bank_i % len(subset)]
        self.bank_i += 1
        return self.banks[b]

    def rotbuf(self, key, n, shape, dt):
        if key not in self.rot:
            self.rot[key] = [[self.alloc(shape, dt, key) for _ in range(n)], 0]
        ent = self.rot[key]
        b = ent[0][ent[1] % n]
        ent[1] += 1
        return b

    def droprot(self, *keys):
        for k_ in keys:
            self.rot.pop(k_, None)


class WS:
    def __init__(self, kb, nslot):
        self.kb = kb
        self.slots = []
        for i in range(nslot):
            off = kb.top
            a = kb.alloc([128, 16, 512], BF16, "wsa")
            b = Buf(kb.nc.alloc_sbuf_tensor_at("wsb_%d" % i, [128, 4, 2048], BF16, offset=off), a.r)
            self.slots.append((a, b, kb.p.stream("w")))
        self.i = 0

    def load(self, src, kc, ncols, wide=False):
        a, b, st = self.slots[self.i % len(self.slots)]
        self.i += 1
        tgt = b if wide else a
        self.kb.p.add("pool", lambda e: e.dma_start(out=tgt.t[:, 0:kc, 0:ncols], in_=src), writes=[tgt.r], stream=st)
        return tgt


def run_blocks(ws, tasks, depth=2):
    h = {}
    n = len(tasks)
    for i in range(min(depth, n)):
        h[i] = ws.load(*tasks[i][0])
    for i in range(n):
        tasks[i][1](h.pop(i))
        if i + depth < n:
            h[i + depth] = ws.load(*tasks[i + depth][0])


def wview(w_ap, c0, nc_):
    return w_ap.rearrange("(k p) n -> p k n", p=128)[:, :, c0:c0 + nc_]


def proj(kb, wb, kcn, col0, x, evac, ncol=128, banks=None):
    p = kb.p
    for ti, (t0, tn) in enumerate(TILES):
        bk = kb.bank(banks)
        for k_ in range(kcn):
            p.add("pe", lambda e, k_=k_, bk=bk, t0=t0, tn=tn: e.matmul(
                bk.t[0:ncol, 0:tn], lhsT=wb.t[:, k_, col0:col0 + ncol], rhs=x.t[:, k_, t0:t0 + tn],
                start=(k_ == 0), stop=(k_ == kcn - 1)), reads=[wb.r, x.r], writes=[bk.r])
        evac(ti, t0, tn, bk)


def rms_fm(kb, get_chunk, kcn, gains, out, ones, tagp):
    p = kb.p
    dtot = kcn * 128
    bks = [kb.bank() for _ in TILES]
    for k_ in range(kcn):
        src = get_chunk(k_)
        sq = kb.rotbuf(tagp + "sq", 2, [128, TX], BF16)
        p.add("act", lambda e, src=src, sq=sq: e.activation(out=sq.t[:, :], in_=src.t[:, 0:TX], func=AF.Square),
              reads=[src.r], writes=[sq.r])
        for ti, (t0, tn) in enumerate(TILES):
            p.add("pe", lambda e, ti=ti, t0=t0, tn=tn, sq=sq, k_=k_: e.matmul(
                bks[ti].t[:, 0:tn], lhsT=ones.t[:, :], rhs=sq.t[:, t0:t0 + tn], start=(k_ == 0), stop=(k_ == kcn - 1)),
                reads=[sq.r, ones.r], writes=[bks[ti].r])
    rstd = kb.rotbuf(tagp + "rstd", 1, [128, TX], F32)
    for ti, (t0, tn) in enumerate(TILES):
        p.add("act", lambda e, ti=ti, t0=t0, tn=tn: e.activation(out=rstd.t[:, t0:t0 + tn], in_=bks[ti].t[:, 0:tn], func=AF.Sqrt,
                                                           scale=1.0 / dtot, bias=EPS), reads=[bks[ti].r], writes=[rstd.r])
    p.add("dve", lambda e: e.reciprocal(out=rstd.t[:, :], in_=rstd.t[:, :]), reads=[rstd.r], writes=[rstd.r])
    for k_ in range(kcn):
        src = get_chunk(k_)
        p.add("dve", lambda e, src=src, k_=k_: e.scalar_tensor_tensor(
            out=out.t[:, k_, :], in0=src.t[:, 0:TX], scalar=gains.t[:, k_:k_ + 1], in1=rstd.t[:, :], op0=ALU.mult, op1=ALU.mult),
            reads=[src.r, gains.r, rstd.r], writes=[out.r])


def dram_chunk_loader(kb, src_fm, tag, nbuf=3):
    st = [kb.p.stream("hl") for _ in range(nbuf)]
    cnt = [0]

    def get(k_):
        i = cnt[0] % nbuf
        cnt[0] += 1
        b = kb.rotbuf(tag, nbuf, [128, TX], F32)
        kb.p.add("sp", lambda e: e.dma_start(out=b.t[:, :], in_=src_fm[k_ * 128:(k_ + 1) * 128, :]), writes=[b.r], stream=st[i])
        return b
    return get


def load_const(kb, dram_ap, shape, dt, st, name="c"):
    b = kb.alloc(shape, dt, name)
    kb.p.add("sp", lambda e: e.dma_start(out=b.t[:], in_=dram_ap), writes=[b.r], stream=st)
    return b


def make_ones(kb):
    ones = kb.alloc([128, 128], BF16, "ones")
    kb.p.add("dve", lambda e: e.memset(ones.t[:, :], 1.0), writes=[ones.r])
    return ones


def build_p1():
    kb = KB("p1")
    p = kb.p
    hT = kb.dram("hT", [D, TX], F32, "ExternalInput")
    w1 = kb.dram("w1", [D, 2432], F32, "ExternalInput")
    gmix = kb.dram("gmix", [128, KC], F32, "ExternalInput")
    gqa = kb.dram("gqa", [128, 6], F32, "ExternalInput")
    gkva = kb.dram("gkva", [128, 4], F32, "ExternalInput")
    cs256 = kb.dram("cs256", [128, 2, 512], BF16, "ExternalInput")
    cqn_o = kb.dram("cqn_o", [Q_LORA, TX], BF16, "ExternalOutput")
    ckvn_o = kb.dram("ckvn_o", [KV_LORA, TX], BF16, "ExternalOutput")
    kr2_o = kb.dram("kr2_o", [128, TX], F32, "ExternalOutput")
    uv_o = kb.dram("uv_o", [T, 2048], BF16, "ExternalOutput")

    stc = p.stream("c")
    sto = [p.stream("o") for _ in range(3)]
    ones = make_ones(kb)
    g_mix = load_const(kb, gmix, [128, KC], F32, stc)
    g_qa = load_const(kb, gqa, [128, 6], F32, stc)
    g_kva = load_const(kb, gkva, [128, 4], F32, stc)
    cs = load_const(kb, cs256, [128, 2, 512], BF16, stc)
    ws = WS(kb, 3)
    xn = kb.alloc([128, KC, TX], BF16, "xn")
    aT = kb.alloc([128, 8, TX], BF16, "aT")
    cq = kb.alloc([128, 6, TX], F32, "cq")
    ckv = kb.alloc([128, 4, TX], F32, "ckv")
    cqn = kb.alloc([128, 6, TX], BF16, "cqn")
    ckvn = kb.alloc([128, 4, TX], BF16, "ckvn")
    kr2 = kb.alloc([128, TX], F32, "kr2")

    m_ = kb.mark()
    rms_fm(kb, dram_chunk_loader(kb, hT, "hch"), KC, g_mix, xn, ones, "n1")
    kb.droprot("hch", "n1sq", "n1rstd")
    kb.release(m_)

    evi = [0]

    def evac_copy(dst_ap_fn, dst_res):
        def ev(ti, t0, tn, bk):
            eng = "act" if evi[0] % 2 == 0 else "dve"
            evi[0] += 1
            if eng == "act":
                p.add("act", lambda e: e.activation(out=dst_ap_fn(t0, tn), in_=bk.t[:, 0:tn], func=AF.Copy), reads=[bk.r], writes=[dst_res])
            else:
                p.add("dve", lambda e: e.tensor_copy(out=dst_ap_fn(t0, tn), in_=bk.t[:, 0:tn]), reads=[bk.r], writes=[dst_res])
        return ev

    def chunk_dst(j):
        if j < 8:
            return (lambda t0, tn: aT.t[:, j, t0:t0 + tn]), aT.r
        if j < 14:
            return (lambda t0, tn: cq.t[:, j - 8, t0:t0 + tn]), cq.r
        if j < 18:
            return (lambda t0, tn: ckv.t[:, j - 14, t0:t0 + tn]), ckv.r
        return (lambda t0, tn: kr2.t[:, t0:t0 + tn]), kr2.r

    def uv_phase():
        for tc in range(9):
            c0 = HALO + tc * 128
            n = min(128, HALO + T - c0)
            uvs = kb.rotbuf("uvs", 2, [128, 2048], BF16)
            for g in range(4):
                bk = kb.bank()
                for ic in range(2):
                    p.add("pe", lambda e, bk=bk, g=g, ic=ic, c0=c0, n=n: e.matmul(
                        bk.t[0:n, :], lhsT=aT.t[:, 2 * g + ic, c0:c0 + n], rhs=cs.t[:, ic, :], start=(ic == 0), stop=(ic == 1)),
                        reads=[aT.r, cs.r], writes=[bk.r])
                p.add("act", lambda e, bk=bk, g=g, n=n, uvs=uvs: e.activation(out=uvs.t[0:n, g * 256:(g + 1) * 256], in_=bk.t[0:n, 0:256], func=AF.Copy),
                      reads=[bk.r], writes=[uvs.r])
                p.add("dve", lambda e, bk=bk, g=g, n=n, uvs=uvs: e.tensor_copy(out=uvs.t[0:n, 1024 + g * 256:1024 + (g + 1) * 256], in_=bk.t[0:n, 256:512]),
                      reads=[bk.r], writes=[uvs.r])
            p.add("sp", lambda e, uvs=uvs, tc=tc, n=n: e.dma_start(out=uv_o[tc * 128:tc * 128 + n, :], in_=uvs.t[0:n, :]),
                  reads=[uvs.r], stream=sto[tc % 3])

    tasks = []
    for b in range(5):
        ncols = 512 if b < 4 else 384

        def comp(wb, b=b, ncols=ncols):
            for jj in range(ncols // 128):
                j = b * 4 + jj
                fn, res = chunk_dst(j)
                proj(kb, wb, KC, jj * 128, xn, evac_copy(fn, res))
            if b == 1:
                uv_phase()
        tasks.append(((wview(w1, b * 512, ncols), KC, ncols), comp))
    run_blocks(ws, tasks)

    p.add("sp", lambda e: e.dma_start(out=kr2_o, in_=kr2.t[:, :]), reads=[kr2.r], stream=sto[0])
    rms_fm(kb, lambda k_: Buf(cq.t[:, k_, :], cq.r), 6, g_qa, cqn, ones, "nq")
    p.add("sp", lambda e: e.dma_start(out=cqn_o.rearrange("(k p) t -> p k t", p=128), in_=cqn.t[:]), reads=[cqn.r], stream=sto[1])
    rms_fm(kb, lambda k_: Buf(ckv.t[:, k_, :], ckv.r), 4, g_kva, ckvn, ones, "nq")
    p.add("sp", lambda e: e.dma_start(out=ckvn_o.rearrange("(k p) t -> p k t", p=128), in_=ckvn.t[:]), reads=[ckvn.r], stream=sto[2])
    p.emit(final_wait_streams=sto)
    return kb.nc


def build_p2():
    kb = KB("p2")
    p = kb.p
    cqnT = kb.dram("cqnT", [Q_LORA, L], BF16, "ExternalInput")
    ckvnT = kb.dram("ckvnT", [KV_LORA, L], BF16, "ExternalInput")
    kr2T = kb.dram("kr2T", [128, L], F32, "ExternalInput")
    cs2 = kb.dram("cs2", [128, L], F32, "ExternalInput")
    wuq = kb.dram("wuq", [Q_LORA, 512], F32, "ExternalInput")
    wukv = kb.dram("wukv", [KV_LORA, 512], F32, "ExternalInput")
    gv = kb.dram("gv", [128, 8], F32, "ExternalInput")
    fold = kb.dram("fold", [128, 64], BF16, "ExternalInput")
    oT = kb.dram("oT", [256, L], BF16, "ExternalOutput")

    stc = p.stream("c")
    sto = [p.stream("o") for _ in range(2)]
    ones = make_ones(kb)
    g = load_const(kb, gv, [128, 8], F32, stc)
    fo = load_const(kb, fold, [128, 64], BF16, stc)
    wq = kb.alloc([128, 6, 512], BF16, "wq")
    wk = kb.alloc([128, 4, 512], BF16, "wk")
    stw = p.stream("w")
    p.add("pool", lambda e: e.dma_start(out=wq.t[:], in_=wuq.rearrange("(k p) n -> p k n", p=128)), writes=[wq.r], stream=stw)
    p.add("pool", lambda e: e.dma_start(out=wk.t[:], in_=wukv.rearrange("(k p) n -> p k n", p=128)), writes=[wk.r], stream=stw)

    Kn = [kb.alloc([128, L], BF16, "Kn") for _ in range(2)]
    V = [kb.alloc([128, NKC, 128], BF16, "V") for _ in range(2)]
    KRh = [kb.alloc([64, L], BF16, "KRh") for _ in range(2)]
    B_S = [0, 1]
    B_O = [2, 3]
    B_D = [4, 5]
    B_P = [6]
    B_M = [7]

    m0 = kb.mark()
    KR = kb.alloc([64, L], F32, "KR")
    sqkr = kb.alloc([64, L], BF16, "sqkr")
    stl = [p.stream("l") for _ in range(4)]
    qtiles = [(i * QT, min(QT, L - i * QT)) for i in range(NQT)]
    for i, (t0, tn) in enumerate(qtiles):
        s = i % 2
        kr = kb.rotbuf("krl", 2, [64, QT], F32)
        krs = kb.rotbuf("krsl", 2, [64, QT], F32)
        cst = kb.rotbuf("cstl", 2, [64, QT], F32)
        snt = kb.rotbuf("sntl", 2, [64, QT], F32)
        p.add("sp", lambda e: e.dma_start(out=kr.t[:, 0:tn], in_=kr2T[0:64, t0:t0 + tn]), writes=[kr.r], stream=stl[s])
        p.add("sp", lambda e: e.dma_start(out=krs.t[:, 0:tn], in_=kr2T[64:128, t0:t0 + tn]), writes=[krs.r], stream=stl[s])
        p.add("sp", lambda e: e.dma_start(out=cst.t[:, 0:tn], in_=cs2[0:64, t0:t0 + tn]), writes=[cst.r], stream=stl[2 + s])
        p.add("sp", lambda e: e.dma_start(out=snt.t[:, 0:tn], in_=cs2[64:128, t0:t0 + tn]), writes=[snt.r], stream=stl[2 + s])
        t1 = kb.rotbuf("krt1", 2, [64, QT], F32)
        t2 = kb.rotbuf("krt2", 2, [64, QT], F32)
        p.add("dve", lambda e: e.scalar_tensor_tensor(
            out=t1.t[:, 0:tn], in0=kr.t[:, 0:tn], scalar=g.t[0:64, 3:4], in1=cst.t[:, 0:tn], op0=ALU.mult, op1=ALU.mult),
            reads=[kr.r, cst.r, g.r], writes=[t1.r])
        p.add("dve", lambda e: e.scalar_tensor_tensor(
            out=t2.t[:, 0:tn], in0=krs.t[:, 0:tn], scalar=g.t[0:64, 4:5], in1=snt.t[:, 0:tn], op0=ALU.mult, op1=ALU.mult),
            reads=[krs.r, snt.r, g.r], writes=[t2.r])
        p.add("dve", lambda e: e.tensor_tensor(out=KR.t[:, t0:t0 + tn], in0=t1.t[:, 0:tn], in1=t2.t[:, 0:tn], op=ALU.add),
              reads=[t1.r, t2.r], writes=[KR.r])
        p.add("act", lambda e: e.activation(out=sqkr.t[:, t0:t0 + tn], in_=kr.t[:, 0:tn], func=AF.Square),
              reads=[kr.r], writes=[sqkr.r])
    for i, (t0, tn) in enumerate(qtiles):
        ck = kb.rotbuf("ckl", 2, [128, 4, QT], BF16)
        p.add("sp", lambda e: e.dma_start(out=ck.t[:, :, 0:tn], in_=ckvnT.rearrange("(k p) t -> p k t", p=128)[:, :, t0:t0 + tn]),
              writes=[ck.r], stream=stl[i % 2])
        for hh in range(2):
            bk = kb.bank(B_P + B_S)
            for r_ in range(4):
                p.add("pe", lambda e: e.matmul(
                    bk.t[:, 0:tn], lhsT=wk.t[:, r_, hh * 256:hh * 256 + 128], rhs=ck.t[:, r_, 0:tn], start=(r_ == 0), stop=(r_ == 3)),
                    reads=[wk.r, ck.r], writes=[bk.r])
            kraw = kb.rotbuf("kraw", 2, [128, QT], F32)
            sq = kb.rotbuf("ksq", 2, [128, QT], BF16)
            p.add("act", lambda e: e.activation(out=kraw.t[:, 0:tn], in_=bk.t[:, 0:tn], func=AF.Copy, scale=g.t[:, 2:3]),
                  reads=[bk.r, g.r], writes=[kraw.r])
            p.add("act", lambda e: e.activation(out=sq.t[:, 0:tn], in_=bk.t[:, 0:tn], func=AF.Square), reads=[bk.r], writes=[sq.r])
            bq_ = kb.bank(B_M)
            p.add("pe", lambda e: e.matmul(bq_.t[:, 0:tn], lhsT=ones.t[:, :], rhs=sq.t[:, 0:tn], start=True, stop=False),
                  reads=[ones.r, sq.r], writes=[bq_.r])
            p.add("pe", lambda e: e.matmul(bq_.t[:, 0:tn], lhsT=ones.t[0:64, :], rhs=sqkr.t[:, t0:t0 + tn], start=False, stop=True),
                  reads=[ones.r, sqkr.r], writes=[bq_.r])
            krstd = kb.rotbuf("krstd", 2, [128, QT], F32)
            p.add("act", lambda e: e.activation(out=krstd.t[:, 0:tn], in_=bq_.t[:, 0:tn], func=AF.Sqrt, scale=1.0 / 192, bias=EPS),
                  reads=[bq_.r], writes=[krstd.r])
            p.add("dve", lambda e: e.reciprocal(out=krstd.t[:, 0:tn], in_=krstd.t[:, 0:tn]), reads=[krstd.r], writes=[krstd.r])
            p.add("dve", lambda e: e.tensor_tensor(out=Kn[hh].t[:, t0:t0 + tn], in0=kraw.t[:, 0:tn], in1=krstd.t[:, 0:tn], op=ALU.mult),
                  reads=[kraw.r, krstd.r], writes=[Kn[hh].r])
            p.add("dve", lambda e: e.tensor_tensor(out=KRh[hh].t[:, t0:t0 + tn], in0=KR.t[:, t0:t0 + tn], in1=krstd.t[0:64, 0:tn], op=ALU.mult),
                  reads=[KR.r, krstd.r], writes=[KRh[hh].r])
            bv = kb.bank(B_O + B_D)
            ncs = (tn + 127) // 128
            for c in range(ncs):
                n = min(128, tn - c * 128)
                for r_ in range(4):
                    p.add("pe", lambda e: e.matmul(
                        bv.t[0:n, c * 128:(c + 1) * 128], lhsT=ck.t[:, r_, c * 128:c * 128 + n], rhs=wk.t[:, r_, hh * 256 + 128:hh * 256 + 256],
                        start=(r_ == 0), stop=(r_ == 3)), reads=[ck.r, wk.r], writes=[bv.r])
            c0 = t0 // 128
            nfull = tn // 128
            if nfull > 0:
                p.add("dve", lambda e: e.tensor_copy(
                    out=V[hh].t[:, c0:c0 + nfull, :], in_=bv.t[:, 0:nfull * 128].rearrange("p (c d) -> p c d", d=128)), reads=[bv.r], writes=[V[hh].r])
            if tn % 128:
                n = tn % 128
                p.add("dve", lambda e: e.tensor_copy(
                    out=V[hh].t[0:n, c0 + nfull, :], in_=bv.t[0:n, nfull * 128:(nfull + 1) * 128]), reads=[bv.r], writes=[V[hh].r])
    kb.droprot("krl", "krsl", "cstl", "sntl", "krt1", "krt2", "ckl", "ksq", "kraw", "krstd")
    kb.release(m0)

    stq = [p.stream("q") for _ in range(4)]
    for qi, (t0, tn) in enumerate(qtiles):
        if _P2_CUT is not None and qi >= _P2_CUT:
            break
        cqt = kb.rotbuf("cqt", 2, [128, 6, QT], BF16)
        cst = kb.rotbuf("cst", 2, [128, QT], F32)
        p.add("sp", lambda e, cqt=cqt, t0=t0, tn=tn: e.dma_start(out=cqt.t[:, :, 0:tn], in_=cqnT.rearrange("(k p) t -> p k t", p=128)[:, :, t0:t0 + tn]),
              writes=[cqt.r], stream=stq[qi % 2])
        p.add("sp", lambda e, cst=cst, t0=t0, tn=tn: e.dma_start(out=cst.t[:, 0:tn], in_=cs2[:, t0:t0 + tn]), writes=[cst.r], stream=stq[2 + qi % 2])
        Qn = []
        QR = []
        for hh in range(2):
            bq = kb.bank(B_P)
            br = kb.bank(B_M)
            for r_ in range(6):
                p.add("pe", lambda e, bq=bq, cqt=cqt, r_=r_, hh=hh, tn=tn: e.matmul(
                    bq.t[:, 0:tn], lhsT=wq.t[:, r_, hh * 256:hh * 256 + 128], rhs=cqt.t[:, r_, 0:tn], start=(r_ == 0), stop=(r_ == 5)),
                    reads=[wq.r, cqt.r], writes=[bq.r])
            for r_ in range(6):
                p.add("pe", lambda e, br=br, cqt=cqt, r_=r_, hh=hh, tn=tn: e.matmul(
                    br.t[:, 0:tn], lhsT=wq.t[:, r_, hh * 256 + 128:hh * 256 + 256], rhs=cqt.t[:, r_, 0:tn], start=(r_ == 0), stop=(r_ == 5)),
                    reads=[wq.r, cqt.r], writes=[br.r])
            sqn = kb.rotbuf("sqn", 2, [128, QT], BF16)
            sqr = kb.rotbuf("sqr", 2, [64, QT], BF16)
            qraw = kb.rotbuf("qraw", 2, [128, QT], F32)
            tmp = kb.rotbuf("qtmp", 2, [128, QT], BF16)
            p.add("act", lambda e, bq=bq, sqn=sqn, tn=tn: e.activation(out=sqn.t[:, 0:tn], in_=bq.t[:, 0:tn], func=AF.Square), reads=[bq.r], writes=[sqn.r])
            p.add("act", lambda e, br=br, sqr=sqr, tn=tn: e.activation(out=sqr.t[:, 0:tn], in_=br.t[0:64, 0:tn], func=AF.Square), reads=[br.r], writes=[sqr.r])
            p.add("act", lambda e, bq=bq, qraw=qraw, tn=tn: e.activation(out=qraw.t[:, 0:tn], in_=bq.t[:, 0:tn], func=AF.Copy, scale=g.t[:, 0:1]),
                  reads=[bq.r, g.r], writes=[qraw.r])
            p.add("dve", lambda e, br=br, cst=cst, tmp=tmp, tn=tn: e.scalar_tensor_tensor(
                out=tmp.t[:, 0:tn], in0=br.t[:, 0:tn], scalar=g.t[:, 1:2], in1=cst.t[:, 0:tn], op0=ALU.mult, op1=ALU.mult),
                reads=[br.r, cst.r, g.r], writes=[tmp.r])
            p.add("pe", lambda e, bq=bq, sqn=sqn, tn=tn: e.matmul(bq.t[:, 0:tn], lhsT=ones.t[:, :], rhs=sqn.t[:, 0:tn], start=True, stop=False),
                  reads=[ones.r, sqn.r], writes=[bq.r])
            p.add("pe", lambda e, bq=bq, sqr=sqr, tn=tn: e.matmul(bq.t[:, 0:tn], lhsT=ones.t[0:64, :], rhs=sqr.t[:, 0:tn], start=False, stop=True),
                  reads=[ones.r, sqr.r], writes=[bq.r])
            p.add("pe", lambda e, br=br, tmp=tmp, tn=tn: e.matmul(br.t[0:64, 0:tn], lhsT=fo.t[:, :], rhs=tmp.t[:, 0:tn], start=True, stop=True),
                  reads=[fo.r, tmp.r], writes=[br.r])
            rstd = kb.rotbuf("qrstd", 2, [128, QT], F32)
            p.add("act", lambda e, bq=bq, rstd=rstd, tn=tn: e.activation(out=rstd.t[:, 0:tn], in_=bq.t[:, 0:tn], func=AF.Sqrt, scale=1.0 / 192, bias=EPS),
                  reads=[bq.r], writes=[rstd.r])
            p.add("dve", lambda e, rstd=rstd, tn=tn: e.reciprocal(out=rstd.t[:, 0:tn], in_=rstd.t[:, 0:tn]), reads=[rstd.r], writes=[rstd.r])
            qn = kb.rotbuf("Qn", 3, [128, QT], BF16)
            qr = kb.rotbuf("QR", 3, [64, QT], BF16)
            p.add("dve", lambda e, qraw=qraw, rstd=rstd, qn=qn, tn=tn: e.tensor_tensor(out=qn.t[:, 0:tn], in0=qraw.t[:, 0:tn], in1=rstd.t[:, 0:tn], op=ALU.mult),
                  reads=[qraw.r, rstd.r], writes=[qn.r])
            p.add("dve", lambda e, br=br, rstd=rstd, qr=qr, tn=tn: e.tensor_tensor(out=qr.t[:, 0:tn], in0=br.t[0:64, 0:tn], in1=rstd.t[0:64, 0:tn], op=ALU.mult),
                  reads=[br.r, rstd.r], writes=[qr.r])
            Qn.append(qn)
            QR.append(qr)
        for hh in range(2):
            bo = kb.banks[B_O[hh]]
            bd = kb.banks[B_D[hh]]
            pts = {}

            def s_stage(c, hh=hh):
                n = min(128, L - c * 128)
                bs = kb.bank(B_S)
                p.add("pe", lambda e: e.matmul(bs.t[0:n, 0:tn], lhsT=Kn[hh].t[:, c * 128:c * 128 + n], rhs=Qn[hh].t[:, 0:tn], start=True, stop=False),
                      reads=[Kn[hh].r, Qn[hh].r], writes=[bs.r])
                p.add("pe", lambda e: e.matmul(bs.t[0:n, 0:tn], lhsT=KRh[hh].t[:, c * 128:c * 128 + n], rhs=QR[hh].t[:, 0:tn], start=False, stop=True),
                      reads=[KRh[hh].r, QR[hh].r], writes=[bs.r])
                pt = kb.rotbuf("pt", 3, [128, QT], BF16)
                p.add("act", lambda e: e.activation(out=pt.t[0:n, 0:tn], in_=bs.t[0:n, 0:tn], func=AF.Exp, scale=1.0 / math.sqrt(192.0)),
                      reads=[bs.r], writes=[pt.r])
                pts[c] = (pt, n)

            def pv_stage(c, hh=hh):
                pt, n = pts.pop(c)
                p.add("pe", lambda e: e.matmul(bo.t[:, 0:tn], lhsT=V[hh].t[0:n, c, :], rhs=pt.t[0:n, 0:tn], start=(c == 0), stop=(c == nkc_ - 1)),
                      reads=[V[hh].r, pt.r], writes=[bo.r])
                p.add("pe", lambda e: e.matmul(bd.t[:, 0:tn], lhsT=ones.t[0:n, :], rhs=pt.t[0:n, 0:tn], start=(c == 0), stop=(c == nkc_ - 1)),
                      reads=[ones.r, pt.r], writes=[bd.r])
            nkc_ = NKC if _P2_NKC is None else _P2_NKC
            if nkc_ == 0:
                p.add("pe", lambda e: e.matmul(bo.t[:, 0:tn], lhsT=ones.t[:, :], rhs=Qn[hh].t[:, 0:tn], start=True, stop=True), reads=[ones.r, Qn[hh].r], writes=[bo.r])
                p.add("pe", lambda e: e.matmul(bd.t[:, 0:tn], lhsT=ones.t[0:64, :], rhs=QR[hh].t[:, 0:tn], start=True, stop=True), reads=[ones.r, QR[hh].r], writes=[bd.r])
            else:
                s_stage(0)
                for c in range(nkc_):
                    if c + 1 < nkc_:
                        s_stage(c + 1)
                    pv_stage(c)
            rden = kb.rotbuf("rden", 2, [128, QT], F32)
            ob = kb.rotbuf("ob", 2, [128, QT], BF16)
            p.add("dve", lambda e, bd=bd, rden=rden: e.reciprocal(out=rden.t[:, 0:tn], in_=bd.t[:, 0:tn]), reads=[bd.r], writes=[rden.r])
            p.add("dve", lambda e, bo=bo, rden=rden, ob=ob: e.tensor_tensor(out=ob.t[:, 0:tn], in0=bo.t[:, 0:tn], in1=rden.t[:, 0:tn], op=ALU.mult),
                  reads=[bo.r, rden.r], writes=[ob.r])
            p.add("sp", lambda e, ob=ob, hh=hh: e.dma_start(out=oT[hh * 128:(hh + 1) * 128, t0:t0 + tn], in_=ob.t[:, 0:tn]), reads=[ob.r], stream=sto[hh])
    p.emit(final_wait_streams=sto)
    return kb.nc


def build_p3():
    kb = KB("p3")
    p = kb.p
    hT = kb.dram("hT", [D, TX], F32, "ExternalInput")
    oTx = kb.dram("oTx", [D, TX], BF16, "ExternalInput")
    UV = kb.dram("UV", [L, 2048], BF16, "ExternalInput")
    csl = kb.dram("csl", [L, 2, TX], BF16, "ExternalInput")
    w3 = kb.dram("w3", [D, 9216], F32, "ExternalInput")
    wpa = kb.dram("wpa", [1024, D], F32, "ExternalInput")
    wpb = kb.dram("wpb", [D, D], F32, "ExternalInput")
    wpc = kb.dram("wpc", [1024, D], F32, "ExternalInput")
    wo = kb.dram("wo", [D, D], F32, "ExternalInput")
    wup = kb.dram("wup", [D, 2 * D_FF], F32, "ExternalInput")
    wdn = kb.dram("wdn", [D_FF, D], F32, "ExternalInput")
    gmix = kb.dram("gmix", [128, KC], F32, "ExternalInput")
    gffn = kb.dram("gffn", [128, KC], F32, "ExternalInput")
    cvc = kb.dram("cvc", [128, 8, 3], F32, "ExternalInput")
    cvf = kb.dram("cvf", [128, 88, 3], F32, "ExternalInput")
    hout = kb.dram("hout", [D, T], F32, "ExternalOutput")
    hmid_d = kb.dram("hmid_d", [D, TX], F32, "Internal")

    stc = p.stream("c")
    sto = [p.stream("o") for _ in range(3)]
    ones = make_ones(kb)
    g_mix = load_const(kb, gmix, [128, KC], F32, stc)
    g_ffn = load_const(kb, gffn, [128, KC], F32, stc)
    cv_c = load_const(kb, cvc, [128, 8, 3], F32, stc)
    cv_f = load_const(kb, cvf, [128, 88, 3], F32, stc)
    ws = WS(kb, 3)
    evi = [0]

    def conv3(xb, cw, ch_i, out_ap, out_res, tmp):
        p.add("act", lambda e: e.activation(out=tmp.t[:, 0:TX], in_=xb.t[:, 0:TX], func=AF.Copy, scale=cw.t[:, ch_i, 0:1]),
              reads=[xb.r, cw.r], writes=[tmp.r])
        p.add("dve", lambda e: e.scalar_tensor_tensor(out=tmp.t[:, 0:TX], in0=xb.t[:, 1:TX + 1], scalar=cw.t[:, ch_i, 1:2], in1=tmp.t[:, 0:TX],
                                                      op0=ALU.mult, op1=ALU.add), reads=[xb.r, cw.r, tmp.r], writes=[tmp.r])
        p.add("dve", lambda e: e.scalar_tensor_tensor(out=out_ap, in0=xb.t[:, 2:TX + 2], scalar=cw.t[:, ch_i, 2:3], in1=tmp.t[:, 0:TX],
                                                      op0=ALU.mult, op1=ALU.add), reads=[xb.r, cw.r, tmp.r], writes=[out_res])

    def padded(key, n):
        fresh = key not in kb.rot
        b = kb.rotbuf(key, n, [128, TX + 2], F32)
        if fresh:
            for bb in kb.rot[key][0]:
                p.add("dve", lambda e, bb=bb: e.memset(bb.t[:, :], 0.0), writes=[bb.r])
        return b

    mM = kb.mark()
    xn = kb.alloc([128, KC, TX], BF16, "xn")
    merged = kb.alloc([128, KC, TX], BF16, "merged")
    m_ = kb.mark()
    rms_fm(kb, dram_chunk_loader(kb, hT, "hch"), KC, g_mix, xn, ones, "n1")
    kb.droprot("hch", "n1sq", "n1rstd")
    kb.release(m_)

    def gate_and_merge(wg, jj, j, ysrc, ykc, wy, first):
        gs = kb.rotbuf("gs", 2, [128, TX], BF16)

        def ev_g(ti, t0, tn, bk):
            p.add("act", lambda e: e.activation(out=gs.t[:, t0:t0 + tn], in_=bk.t[:, 0:tn], func=AF.Sigmoid), reads=[bk.r], writes=[gs.r])
        proj(kb, wg, KC, jj * 128, xn, ev_g)

        def ev_y(ti, t0, tn, bk):
            if first:
                p.add("dve", lambda e: e.tensor_tensor(out=merged.t[:, j, t0:t0 + tn], in0=bk.t[:, 0:tn], in1=gs.t[:, t0:t0 + tn], op=ALU.mult),
                      reads=[bk.r, gs.r], writes=[merged.r])
            else:
                t1 = kb.rotbuf("mt1", 3, [128, 344], F32)
                p.add("dve", lambda e: e.tensor_tensor(out=t1.t[:, 0:tn], in0=bk.t[:, 0:tn], in1=gs.t[:, t0:t0 + tn], op=ALU.mult),
                      reads=[bk.r, gs.r], writes=[t1.r])
                p.add("dve", lambda e: e.tensor_tensor(out=merged.t[:, j, t0:t0 + tn], in0=t1.t[:, 0:tn], in1=merged.t[:, j, t0:t0 + tn], op=ALU.add),
                      reads=[t1.r, merged.r], writes=[merged.r])
        proj(kb, wy, ykc, jj * 128, ysrc, ev_y)

    def merge_pass(br, ysrc, ykc, wy_dram, first):
        tasks = []
        for jb in range(4):
            hold = {}

            def comp_g(wb, jb=jb, hold=hold):
                hold["g"] = wb

            def comp_y(wb, jb=jb, hold=hold):
                for jj in range(4):
                    gate_and_merge(hold["g"], jj, jb * 4 + jj, ysrc, ykc, wb, first)
            tasks.append(((wview(w3, 3072 + br * 2048 + jb * 512, 512), KC, 512), comp_g))
            tasks.append(((wview(wy_dram, jb * 512, 512), ykc, 512), comp_y))
        return tasks

    mC = kb.mark()
    ycin = kb.alloc([128, 8, TX], BF16, "ycin")
    ccs = kb.alloc([128, 4, TX], F32, "ccs")
    cvb = kb.alloc([128, 4, TX], F32, "cvb")
    tasks = []
    for half in range(2):
        def comp_cc(wb, half=half):
            for jj in range(4):
                def ev(ti, t0, tn, bk, jj=jj):
                    p.add("act", lambda e: e.activation(out=ccs.t[:, jj, t0:t0 + tn], in_=bk.t[:, 0:tn], func=AF.Copy), reads=[bk.r], writes=[ccs.r])
                proj(kb, wb, KC, jj * 128, xn, ev)

        def comp_ch(wb, half=half):
            for jj in range(4):
                xb = padded("cxb", 2)

                def ev(ti, t0, tn, bk, jj=jj, xb=xb):
                    p.add("dve", lambda e: e.tensor_tensor(out=xb.t[:, 1 + t0:1 + t0 + tn], in0=bk.t[:, 0:tn], in1=ccs.t[:, jj, t0:t0 + tn], op=ALU.mult),
                          reads=[bk.r, ccs.r], writes=[xb.r])
                proj(kb, wb, KC, jj * 128, xn, ev)
                tmp = kb.rotbuf("ctmp", 2, [128, TX], F32)
                conv3(xb, cv_c, half * 4 + jj, cvb.t[:, jj, :], cvb.r, tmp)

        def comp_cb(wb, half=half):
            for jj in range(4):
                def ev(ti, t0, tn, bk, jj=jj):
                    p.add("dve", lambda e: e.tensor_tensor(out=ycin.t[:, half * 4 + jj, t0:t0 + tn], in0=bk.t[:, 0:tn], in1=cvb.t[:, jj, t0:t0 + tn], op=ALU.mult),
                          reads=[bk.r, cvb.r], writes=[ycin.r])
                proj(kb, wb, KC, jj * 128, xn, ev)
        tasks.append(((wview(w3, 1024 + half * 512, 512), KC, 512), comp_cc))
        tasks.append(((wview(w3, 2048 + half * 512, 512), KC, 512), comp_ch))
        tasks.append(((wview(w3, 0 + half * 512, 512), KC, 512), comp_cb))
    tasks += merge_pass(2, ycin, 8, wpc, True)
    run_blocks(ws, tasks)
    kb.droprot("cxb", "ctmp", "gs", "mt1")
    kb.release(mC)

    mA = kb.mark()
    fa = kb.alloc([128, 8, TX], BF16, "fa")
    stf = [p.stream("f") for _ in range(6)]
    fi = 0
    for ti, (t0, tn) in enumerate(TILES):
        for c in range(NKC):
            n = min(128, L - c * 128)
            uvt = kb.rotbuf("uvt", 3, [128, 2048], BF16)
            cst = kb.rotbuf("cslt", 3, [128, 2, 344], BF16)
            s = fi % 3
            fi += 1
            p.add("sp", lambda e, uvt=uvt, c=c, n=n: e.dma_start(out=uvt.t[0:n, :], in_=UV[c * 128:c * 128 + n, :]), writes=[uvt.r], stream=stf[s])
            p.add("sp", lambda e, cst=cst, c=c, n=n, t0=t0, tn=tn: e.dma_start(out=cst.t[0:n, :, 0:tn], in_=csl[c * 128:c * 128 + n, :, t0:t0 + tn]),
                  writes=[cst.r], stream=stf[3 + s])
            for j in range(8):
                bk = kb.banks[j]
                p.add("pe", lambda e, bk=bk, uvt=uvt, cst=cst, j=j, n=n, c=c, tn=tn: e.matmul(
                    bk.t[:, 0:tn], lhsT=uvt.t[0:n, j * 128:(j + 1) * 128], rhs=cst.t[0:n, 0, 0:tn], start=(c == 0), stop=False),
                    reads=[uvt.r, cst.r], writes=[bk.r])
                p.add("pe", lambda e, bk=bk, uvt=uvt, cst=cst, j=j, n=n, c=c, tn=tn: e.matmul(
                    bk.t[:, 0:tn], lhsT=uvt.t[0:n, 1024 + j * 128:1024 + (j + 1) * 128], rhs=cst.t[0:n, 1, 0:tn], start=False, stop=(c == NKC - 1)),
                    reads=[uvt.r, cst.r], writes=[bk.r])
        for j in range(8):
            bk = kb.banks[j]
            if j % 2 == 0:
                p.add("act", lambda e, bk=bk, j=j, t0=t0, tn=tn: e.activation(out=fa.t[:, j, t0:t0 + tn], in_=bk.t[:, 0:tn], func=AF.Copy), reads=[bk.r], writes=[fa.r])
            else:
                p.add("dve", lambda e, bk=bk, j=j, t0=t0, tn=tn: e.tensor_copy(out=fa.t[:, j, t0:t0 + tn], in_=bk.t[:, 0:tn]), reads=[bk.r], writes=[fa.r])
    run_blocks(ws, merge_pass(0, fa, 8, wpa, False))
    kb.droprot("uvt", "cslt", "gs", "mt1")
    kb.release(mA)

    mB = kb.mark()
    ot = kb.alloc([128, KC, TX], BF16, "ot")
    p.add("sp", lambda e: e.dma_start(out=ot.t[:], in_=oTx.rearrange("(k p) t -> p k t", p=128)), writes=[ot.r], stream=stc)
    run_blocks(ws, merge_pass(1, ot, KC, wpb, False))
    kb.droprot("gs", "mt1")
    kb.release(mB)

    get_h = dram_chunk_loader(kb, hT, "hch2")
    tasks = []
    for ob in range(4):
        def comp(wb, ob=ob):
            for jj in range(4):
                oc = ob * 4 + jj
                hc = get_h(oc)
                hm = kb.rotbuf("hm", 2, [128, TX], F32)

                def ev(ti, t0, tn, bk, hc=hc, hm=hm):
                    p.add("dve", lambda e: e.tensor_tensor(out=hm.t[:, t0:t0 + tn], in0=bk.t[:, 0:tn], in1=hc.t[:, t0:t0 + tn], op=ALU.add),
                          reads=[bk.r, hc.r], writes=[hm.r])
                proj(kb, wb, KC, jj * 128, merged, ev)
                p.add("sp", lambda e, hm=hm, oc=oc: e.dma_start(out=hmid_d[oc * 128:(oc + 1) * 128, :], in_=hm.t[:, :]), reads=[hm.r], stream=sto[oc % 3])
        tasks.append(((wview(wo, ob * 512, 512), KC, 512), comp))
    run_blocks(ws, tasks)
    kb.droprot("hch", "hch2", "hm", "n1sq", "n1rstd")
    kb.release(mM)

    hm = kb.alloc([128, KC, TX], F32, "hmid")
    xn2 = kb.alloc([128, KC, TX], BF16, "xn2")
    sth = [p.stream("h") for _ in range(2)]
    hres = [Res() for _ in range(KC)]
    for k_ in range(KC):
        p.add("sp", lambda e, k_=k_: e.dma_start(out=hm.t[:, k_, :], in_=hmid_d[k_ * 128:(k_ + 1) * 128, :]), writes=[hres[k_]], stream=sth[k_ % 2])
    m_ = kb.mark()
    rms_fm(kb, lambda k_: Buf(hm.t[:, k_, :], hres[k_]), KC, g_ffn, xn2, ones, "n2")
    kb.droprot("n2sq", "n2rstd")
    kb.release(m_)
    tasks = []
    NG = D_FF // 512
    for q in range(NG):
        hold = {}

        def up_chunk(wb, q, which, jj):
            ch_i = which * 44 + q * 4 + jj
            xb = padded("fxb", 2)

            def ev(ti, t0, tn, bk, xb=xb):
                if evi[0] % 2 == 0:
                    p.add("act", lambda e: e.activation(out=xb.t[:, 1 + t0:1 + t0 + tn], in_=bk.t[:, 0:tn], func=AF.Copy), reads=[bk.r], writes=[xb.r])
                else:
                    p.add("dve", lambda e: e.tensor_copy(out=xb.t[:, 1 + t0:1 + t0 + tn], in_=bk.t[:, 0:tn]), reads=[bk.r], writes=[xb.r])
                evi[0] += 1
            proj(kb, wb, KC, jj * 128, xn2, ev)
            tmp = kb.rotbuf("ftmp", 2, [128, TX], F32)
            cv = kb.rotbuf("fcv%d" % which, 2, [128, TX], F32)
            conv3(xb, cv_f, ch_i, cv.t[:, :], cv.r, tmp)
            if which == 0:
                p.add("act", lambda e: e.activation(out=cv.t[:, :], in_=cv.t[:, :], func=AF.Silu), reads=[cv.r], writes=[cv.r])
            return cv

        def comp_a(wb, q=q, hold=hold):
            hold["wa"] = wb

        def comp_b(wb, q=q, hold=hold):
            hid = kb.rotbuf("hid", 2, [128, 4, TX], BF16)
            for jj in range(4):
                a = up_chunk(hold["wa"], q, 0, jj)
                b = up_chunk(wb, q, 1, jj)
                p.add("dve", lambda e: e.tensor_tensor(out=hid.t[:, jj, :], in0=a.t[:, :], in1=b.t[:, :], op=ALU.mult),
                      reads=[a.r, b.r], writes=[hid.r])
            hold["hid"] = hid

        def comp_d(wb, q=q, hold=hold):
            hid = hold["hid"]
            for oc in range(KC):
                for ti, (t0, tn) in enumerate(TILES):
                    bk = kb.bank()
                    for jj in range(4):
                        p.add("pe", lambda e: e.matmul(
                            bk.t[:, 0:tn], lhsT=wb.t[:, jj, oc * 128:(oc + 1) * 128], rhs=hid.t[:, jj, t0:t0 + tn], start=(jj == 0), stop=(jj == 3)),
                            reads=[wb.r, hid.r], writes=[bk.r])
                    p.add("dve", lambda e: e.tensor_tensor(out=hm.t[:, oc, t0:t0 + tn], in0=bk.t[:, 0:tn], in1=hm.t[:, oc, t0:t0 + tn], op=ALU.add),
                          reads=[bk.r, hres[oc]], writes=[hres[oc]])
        tasks.append(((wview(wup, q * 512, 512), KC, 512), comp_a))
        tasks.append(((wview(wup, D_FF + q * 512, 512), KC, 512), comp_b))
        tasks.append(((wdn.rearrange("(k p) n -> p k n", p=128)[:, q * 4:(q + 1) * 4, :], 4, 2048, True), comp_d))
    run_blocks(ws, tasks)
    for k_ in range(KC):
        p.add("sp", lambda e, k_=k_: e.dma_start(out=hout[k_ * 128:(k_ + 1) * 128, :], in_=hm.t[:, k_, HALO:HALO + T]), reads=[hres[k_]], stream=sto[k_ % 3])
    p.emit(final_wait_streams=sto)
    return kb.nc


_CACHE = {}
_OP_LIMIT = None
_P2_CUT = None
_P2_NKC = None
_DBG = None
_STOP = None


def _prog(name):
    if name not in _CACHE:
        _CACHE[name] = {"p1": build_p1, "p2": build_p2, "p3": build_p3}[name]()
    return _CACHE[name]


def _pcol(v, kc):
    return np.ascontiguousarray(v.reshape(kc, 128).T.astype(np.float32))


def _consts():
    if "consts" in _CACHE:
        return _CACHE["consts"]
    c = {}
    i = np.arange(256)
    ang = 2.0 * np.pi * ((i[:, None] * i[None, :]) % 256) / 256.0
    cs = np.concatenate([np.cos(ang), np.sin(ang)], axis=1) / 16.0
    c["cs256"] = np.ascontiguousarray(cs.reshape(2, 128, 512).transpose(1, 0, 2)).astype(NPBF)
    inv = 1.0 / (10000.0 ** (np.arange(0, 64, 2, dtype=np.float32) / 64.0))
    a = np.arange(L, dtype=np.float32)[:, None] * inv[None, :].astype(np.float32)
    cos = np.cos(a).T.astype(np.float32)
    sin = np.sin(a).T.astype(np.float32)
    c["cs2"] = np.ascontiguousarray(np.concatenate([cos, cos, -sin, sin], axis=0))
    fold = np.zeros((128, 64), np.float32)
    fold[np.arange(128), np.arange(128) % 64] = 1.0
    c["fold"] = fold.astype(NPBF)
    t = np.arange(L, dtype=np.int64)
    csl = []
    for ci in range(NCORE):
        m = ci * T - HALO + np.arange(TX, dtype=np.int64)
        valid = (m >= 0) & (m < L)
        ph = 2.0 * np.pi * ((t[:, None] * np.clip(m, 0, L - 1)[None, :]) % L).astype(np.float64) / L
        tab = np.stack([np.cos(ph), -np.sin(ph)], axis=1) / math.sqrt(L)
        tab[:, :, ~valid] = 0.0
        csl.append(tab.astype(NPBF))
    c["csl"] = csl
    _CACHE["consts"] = c
    return c


def _shard_fm(h_full):
    hp = np.zeros((L + 2 * HALO, h_full.shape[1]), h_full.dtype)
    hp[HALO:HALO + L] = h_full
    return [np.ascontiguousarray(hp[ci * T:ci * T + TX].T) for ci in range(NCORE)]


def _run(prog, in_maps):
    return run_bass_kernel_spmd(prog, in_maps, core_ids=list(range(NCORE))).results


def kernel(x, meta_tokens, g_mix, w_in, g_qa, g_kva, w_uq, w_ukv, g_q, g_k, conv_c,
           w_pa, w_pb, w_pc, w_o, g_ffn, w_up, conv_ffn, w_down):
    f32 = np.float32
    cst = _consts()
    h = np.concatenate([np.asarray(meta_tokens, f32), np.asarray(x, f32)[0]], axis=0)
    swap = np.concatenate([np.arange(32, 64), np.arange(0, 32)])
    for l in range(2):
        W = np.asarray(w_in[l], f32)
        hsh = _shard_fm(h)
        krc = W[:, OFF_KR:OFF_KR + 64]
        w1 = np.ascontiguousarray(np.concatenate([W[:, 0:OFF_KR], krc, krc[:, swap]], axis=1))
        gm = _pcol(np.asarray(g_mix[l]), KC)
        common = {"w1": w1, "gmix": gm, "gqa": _pcol(np.asarray(g_qa[l]), 6), "gkva": _pcol(np.asarray(g_kva[l]), 4), "cs256": cst["cs256"]}
        r1 = _run(_prog("p1"), [dict(common, hT=hsh[ci]) for ci in range(NCORE)])
        own = slice(HALO, HALO + T)
        cqnT = np.ascontiguousarray(np.concatenate([np.asarray(r["cqn_o"])[:, own] for r in r1], axis=1))
        ckvnT = np.ascontiguousarray(np.concatenate([np.asarray(r["ckvn_o"])[:, own] for r in r1], axis=1))
        kr2T = np.ascontiguousarray(np.concatenate([np.asarray(r["kr2_o"])[:, own] for r in r1], axis=1))
        UVf = np.ascontiguousarray(np.concatenate([np.asarray(r["uv_o"]) for r in r1], axis=0))
        if _DBG is not None:
            _DBG["p1_%d" % l] = dict(cqnT=cqnT, ckvnT=ckvnT, kr2T=kr2T, UV=UVf)
        if _STOP == ("p1", l):
            return None
        WQ = np.asarray(w_uq[l], f32).reshape(Q_LORA, 16, 192)
        WKV = np.asarray(w_ukv[l], f32).reshape(KV_LORA, 16, 256)
        gq = np.asarray(g_q[l], f32)
        gk = np.asarray(g_k[l], f32)
        gvm = np.zeros((128, 8), f32)
        gvm[:, 0] = gq[:128]
        gvm[:64, 1] = gq[128:]
        gvm[64:, 1] = gq[128:][swap]
        gvm[:, 2] = gk[:128]
        gvm[:64, 3] = gk[128:]
        gvm[:64, 4] = gk[128:][swap]
        maps = []
        for ci in range(NCORE):
            wq_c = np.concatenate([np.concatenate([WQ[:, hd, :128], WQ[:, hd, 128:], WQ[:, hd, 128:][:, swap]], axis=1) for hd in (2 * ci, 2 * ci + 1)], axis=1)
            wkv_c = np.concatenate([WKV[:, hd, :] for hd in (2 * ci, 2 * ci + 1)], axis=1)
            maps.append({"cqnT": cqnT, "ckvnT": ckvnT, "kr2T": kr2T, "cs2": cst["cs2"], "wuq": np.ascontiguousarray(wq_c),
                         "wukv": np.ascontiguousarray(wkv_c), "gv": gvm, "fold": cst["fold"]})
        r2 = _run(_prog("p2"), maps)
        oT_full = np.concatenate([np.asarray(r["oT"]) for r in r2], axis=0)
        if _DBG is not None:
            _DBG["p2_%d" % l] = dict(oT=oT_full)
        if _STOP == ("p2", l):
            return None
        op = np.zeros((D, L + 2 * HALO), oT_full.dtype)
        op[:, HALO:HALO + L] = oT_full
        w3 = np.ascontiguousarray(W[:, OFF_C:])
        common = {"UV": UVf, "w3": w3, "wpa": np.asarray(w_pa[l], f32), "wpb": np.asarray(w_pb[l], f32), "wpc": np.asarray(w_pc[l], f32),
                  "wo": np.asarray(w_o[l], f32), "wup": np.asarray(w_up[l], f32), "wdn": np.asarray(w_down[l], f32),
                  "gmix": gm, "gffn": _pcol(np.asarray(g_ffn[l]), KC),
                  "cvc": np.ascontiguousarray(np.asarray(conv_c[l], f32).reshape(3, 8, 128).transpose(2, 1, 0)),
                  "cvf": np.ascontiguousarray(np.asarray(conv_ffn[l], f32).reshape(3, 88, 128).transpose(2, 1, 0))}
        r3 = _run(_prog("p3"), [dict(common, hT=hsh[ci], oTx=np.ascontiguousarray(op[:, ci * T:ci * T + TX]), csl=cst["csl"][ci]) for ci in range(NCORE)])
        h = np.ascontiguousarray(np.concatenate([np.asarray(r["hout"]) for r in r3], axis=1).T)
        if _DBG is not None:
            _DBG["p3_%d" % l] = dict(h=h)
        if _STOP == ("p3", l):
            return None
    return h[NMETA:][None].astype(np.float32)
```
